# Optimizing a Trainium2 kernel written in Bass

```python
import math
import numpy as np
import jax
import jax.numpy as jnp
from jax import lax

D_MODEL = 2048
BATCH = 16
SEQ = 2048
DEPTH = 1

GLA_HEADS = 4
GLA_KW = D_MODEL // 2
GLA_VW = D_MODEL
GLA_DK = GLA_KW // GLA_HEADS
GLA_DV = GLA_VW // GLA_HEADS
GATE_RANK = 16
GATE_NORM = 16.0
CHUNK = 64
CONV_W = D_MODEL
CONV_K = 3
FFN_HIDDEN = 4 * D_MODEL
EPS = 1e-6

kernel_name = "hybrid_gla_shortconv_gated_merge"

_WIDTHS = (GLA_KW, GLA_KW, GLA_VW, GLA_VW, GATE_RANK, CONV_W, CONV_W, CONV_W, D_MODEL, D_MODEL)
IN_COLS = GLA_KW * 2 + GLA_VW * 2 + GATE_RANK + CONV_W * 3 + D_MODEL * 2


def rms_norm(x, g):
    xf = x.astype(jnp.float32)
    y = xf * lax.rsqrt(jnp.mean(xf * xf, axis=-1, keepdims=True) + EPS)
    return (y * g.astype(jnp.float32)).astype(x.dtype)


def _to_chunks(t):
    b, s, h, d = t.shape
    return jnp.transpose(t.reshape(b, s // CHUNK, CHUNK, h, d), (1, 0, 3, 2, 4))


def gla_chunked(q, k, v, log_a):
    b, s, h, dv = v.shape
    dk = q.shape[-1]
    qc = _to_chunks(q.astype(jnp.float32) * (dk ** -0.5))
    kc = _to_chunks(k.astype(jnp.float32))
    vc = _to_chunks(v.astype(jnp.float32))
    gc = _to_chunks(log_a.astype(jnp.float32))
    causal = jnp.tril(jnp.ones((CHUNK, CHUNK), dtype=bool))

    def step(state, inp):
        q_c, k_c, v_c, g_c = inp
        cum = jnp.cumsum(g_c, axis=2)
        last = cum[:, :, -1:, :]
        inter = jnp.einsum('bhtd,bhde->bhte', q_c * jnp.exp(cum), state)
        diff = cum[:, :, :, None, :] - cum[:, :, None, :, :]
        decay = jnp.exp(jnp.where(causal[None, None, :, :, None], diff, -jnp.inf))
        scores = jnp.einsum('bhtd,bhsd,bhtsd->bhts', q_c, k_c, decay)
        intra = jnp.einsum('bhts,bhse->bhte', scores, v_c)
        new_state = state * jnp.exp(last[:, :, 0, :, None]) + jnp.einsum(
            'bhcd,bhce->bhde', k_c * jnp.exp(last - cum), v_c)
        return new_state, inter + intra

    state0 = jnp.zeros((b, h, dk, dv), jnp.float32)
    _, out = lax.scan(step, state0, (qc, kc, vc, gc))
    return jnp.transpose(out, (1, 0, 3, 2, 4)).reshape(b, s, h, dv)


def causal_depthwise_conv(u, w):
    c = u.shape[-1]
    return lax.conv_general_dilated(
        u, w.reshape(CONV_K, 1, c).astype(u.dtype), window_strides=(1,),
        padding=((CONV_K - 1, 0),), dimension_numbers=('NWC', 'WIO', 'NWC'),
        feature_group_count=c)


def setup_inputs(seed: int = 0) -> dict:
    key = jax.random.key(seed)
    ks = jax.random.split(key, 20)
    n = jax.random.normal
    f = jnp.float32
    return {
        "x": n(ks[0], (BATCH, SEQ, D_MODEL), f),
        "g_mix": 1.0 + 0.02 * n(ks[1], (DEPTH, D_MODEL), f),
        "w_in": n(ks[2], (DEPTH, D_MODEL, IN_COLS), f) * D_MODEL ** -0.5,
        "w_gate_up": n(ks[3], (DEPTH, GATE_RANK, GLA_KW), f) * GATE_RANK ** -0.5,
        "b_gate": 0.02 * n(ks[4], (DEPTH, GLA_KW), f),
        "g_gla_norm": 1.0 + 0.02 * n(ks[5], (DEPTH, GLA_DV), f),
        "w_gla_out": n(ks[6], (DEPTH, GLA_VW, D_MODEL), f) * GLA_VW ** -0.5,
        "conv_w": n(ks[7], (DEPTH, CONV_K, CONV_W), f) * CONV_K ** -0.5,
        "w_conv_out": n(ks[8], (DEPTH, CONV_W, D_MODEL), f) * CONV_W ** -0.5,
        "w_o": n(ks[9], (DEPTH, D_MODEL, D_MODEL), f) * D_MODEL ** -0.5,
        "g_ffn": 1.0 + 0.02 * n(ks[10], (DEPTH, D_MODEL), f),
        "w_ffn_up": n(ks[11], (DEPTH, D_MODEL, FFN_HIDDEN), f) * D_MODEL ** -0.5,
        "w_ffn_down": n(ks[12], (DEPTH, FFN_HIDDEN, D_MODEL), f) * FFN_HIDDEN ** -0.5,
        "g_final": 1.0 + 0.02 * n(ks[13], (D_MODEL,), f),
    }


def reference(x, g_mix, w_in, w_gate_up, b_gate, g_gla_norm, w_gla_out, conv_w,
              w_conv_out, w_o, g_ffn, w_ffn_up, w_ffn_down, g_final):
    b, s, _ = x.shape
    splits = np.cumsum(_WIDTHS)[:-1].tolist()
    for l in range(DEPTH):
        h = rms_norm(x, g_mix[l])
        proj = jnp.einsum('bsd,dn->bsn', h, w_in[l])
        (q, k, v, r, lr, cb, cc, cx, ga, gb) = jnp.split(proj, splits, axis=-1)

        gate_pre = jnp.einsum('bsr,rk->bsk', lr, w_gate_up[l]) + b_gate[l]
        log_a = jax.nn.log_sigmoid(gate_pre.astype(jnp.float32)) / GATE_NORM
        o = gla_chunked(q.reshape(b, s, GLA_HEADS, GLA_DK),
                        k.reshape(b, s, GLA_HEADS, GLA_DK),
                        v.reshape(b, s, GLA_HEADS, GLA_DV),
                        log_a.reshape(b, s, GLA_HEADS, GLA_DK))
        o = rms_norm(o, g_gla_norm[l]).reshape(b, s, GLA_VW).astype(x.dtype)
        y_a = jnp.einsum('bse,ed->bsd', o * jax.nn.silu(r), w_gla_out[l])

        u = causal_depthwise_conv(cc * cx, conv_w[l])
        y_b = jnp.einsum('bsc,cd->bsd', cb * u, w_conv_out[l])

        merged = jax.nn.sigmoid(ga) * y_a + jax.nn.sigmoid(gb) * y_b
        x = x + jnp.einsum('bsd,de->bse', merged, w_o[l])

        h2 = rms_norm(x, g_ffn[l])
        hid = jnp.square(jax.nn.relu(jnp.einsum('bsd,df->bsf', h2, w_ffn_up[l])))
        x = x + jnp.einsum('bsf,fd->bsd', hid, w_ffn_down[l])
    return rms_norm(x, g_final)
```

```python
import numpy as np
import concourse.bass as bass
import concourse.mybir as mybir
from concourse.bass_utils import run_bass_kernel_spmd

F32 = mybir.dt.float32
BF16 = mybir.dt.bfloat16
AF = mybir.ActivationFunctionType
ALU = mybir.AluOpType

D = 2048
KC = 16
T = 512
NSUB = 4
NH = 4
DK = 256
DV = 512
FF = 8192
IN_COLS = 16400
EPS = 1e-6
N_CORES = 8
SEQ = 2048
BATCH = 16

Q_OFF, K_OFF, V_OFF, R_OFF, LR_OFF = 0, 1024, 2048, 4096, 6144
CB_OFF, CC_OFF, CX_OFF, GA_OFF, GB_OFF = 6160, 8208, 10256, 12304, 14352

C_ID, C_TRIU, C_TRIS, C_MASK, C_ONES = 0, 128, 256, 384, 512
C_GMIX, C_GFFN, C_GFIN, C_GGLA, C_CW = 640, 656, 672, 688, 1200
NCONST = 1248

RING = 6
SBUF_BASE = 16512
SBUF_END = 229344


NSLOT = 152
DBG_NAMES = []
WORDER = []


class Sem:
    _n = 0

    def __init__(self, h):
        self.h = h
        self.count = 0
        self.id = Sem._n
        Sem._n += 1


class Eng:
    def __init__(self, h, sem, name, is_pe=False):
        self.h = h
        self.sem = sem
        self.name = name
        self.is_pe = is_pe
        self.waited = {}


class Buf:
    __slots__ = ("lo", "hi", "w", "r", "ov", "excl", "name")

    def __init__(self, name, lo=None, hi=None, excl=False):
        self.name = name
        self.lo = lo
        self.hi = hi
        self.w = {}
        self.r = {}
        self.ov = [self]
        self.excl = excl


class Tracker:
    def __init__(self):
        self.bufs = []

    def buf(self, name, lo=None, size=None, excl=False):
        b = Buf(name, lo, None if lo is None else lo + size, excl)
        if lo is not None:
            for o in self.bufs:
                if o.lo is not None and o.lo < b.hi and b.lo < o.hi:
                    o.ov.append(b)
                    b.ov.append(o)
        self.bufs.append(b)
        return b

    @staticmethod
    def _add(deps, d):
        for k, v in d.items():
            cur = deps.get(k)
            if cur is None or cur[1] < v[1]:
                deps[k] = v

    def _wait(self, eng, reads, writes):
        raw = {}
        other = {}
        for b in reads:
            for o in b.ov:
                self._add(raw, o.w)
                if o.excl:
                    self._add(other, o.r)
        for b in writes:
            for o in b.ov:
                self._add(other, o.w)
                self._add(other, o.r)
        need = {}
        for k, v in raw.items():
            sem, val, e = v
            if e is eng and eng.is_pe:
                continue
            need[k] = v
        for k, v in other.items():
            sem, val, e = v
            if e is eng:
                continue
            cur = need.get(k)
            if cur is None or cur[1] < val:
                need[k] = v
        for k, (sem, val, e) in need.items():
            if eng.waited.get(k, 0) >= val:
                continue
            eng.h.wait_ge(sem.h, val)
            eng.waited[k] = val

    def _record(self, tok, reads, writes):
        k = tok[0].id
        for b in reads:
            cur = b.r.get(k)
            if cur is None or cur[1] < tok[1]:
                b.r[k] = tok
        for b in writes:
            b.w = {k: tok}
            b.r = {}

    def op(self, eng, fn, reads=(), writes=(), signal=True):
        self._wait(eng, reads, writes)
        ins = fn()
        if signal:
            eng.sem.count += 1
            ins.then_inc(eng.sem.h, 1)
            tok = (eng.sem, eng.sem.count, eng)
        else:
            tok = (eng.sem, eng.sem.count + 1, eng)
        self._record(tok, reads, writes)
        return tok

    def dma(self, qeng, dsem, out, in_, reads=(), writes=()):
        self._wait(qeng, reads, writes)
        qeng.h.dma_start(out=out, in_=in_).then_inc(dsem.h, 16)
        dsem.count += 16
        tok = (dsem, dsem.count, None)
        self._record(tok, reads, writes)
        return tok


def build_nc(n_tiles, tiles_per_seq, debug=False):
    n_tok = n_tiles * T
    nc = bass.Bass("TRN2", target_bir_lowering=False)
    xT = nc.dram_tensor("xT", [D, n_tok], F32, kind="ExternalInput").ap()
    wts = nc.dram_tensor("wts", [NSLOT, 128, KC * 256], F32, kind="ExternalInput").ap()
    consts_d = nc.dram_tensor("consts", [128, NCONST], F32, kind="ExternalInput").ap()
    wgu_d = nc.dram_tensor("wgu", [17, 1024], F32, kind="ExternalInput").ap()
    wlr_d = nc.dram_tensor("wlr", [128, KC * 16], F32, kind="ExternalInput").ap()
    outT = nc.dram_tensor("outT", [D, n_tok], F32, kind="ExternalOutput").ap()

    tr = Tracker()
    cur = [SBUF_BASE]

    def alloc(name, shape, dt, at=None, nbuf=True):
        esz = 4 if dt == F32 else 2
        n = 1
        for s in shape[1:]:
            n *= s
        size = (n * esz + 31) // 32 * 32
        if at is None:
            off = cur[0]
            cur[0] += size
            assert cur[0] <= SBUF_END, (name, cur[0])
        else:
            off = at
            assert off + size <= SBUF_END, name
        t = nc.alloc_sbuf_tensor_at(name, list(shape), dt, offset=off)
        b = tr.buf(name, off, size) if nbuf else None
        return t, b, off, size

    def alloc2(name, shape, dt):
        ts_, bs_ = [], []
        for i in range(2):
            t_, b_, _, _ = alloc(f"{name}{i}", shape, dt)
            ts_.append(t_)
            bs_.append(b_)
        return ts_, bs_

    cst, b_cst, _, _ = alloc("cst", [128, NCONST], F32)
    wgu, b_wgu, _, _ = alloc("wgu_s", [32, 1024], F32)
    wlr, b_wlr, _, _ = alloc("wlr_s", [128, KC, 16], BF16)
    lraug, b_lraug, _, _ = alloc("lraug", [32, T], F32)
    state_f, state_b, b_sf, b_sb = [], [], [], []
    for i in range(8):
        t_, b_, _, _ = alloc(f"stf{i}", [128, DV], F32)
        state_f.append(t_)
        b_sf.append(b_)
    for i in range(8):
        t_, b_, _, _ = alloc(f"stb{i}", [128, DV], BF16)
        state_b.append(t_)
        b_sb.append(b_)
    halo, b_halo, _, _ = alloc("halo", [128, 16, 2], F32)
    rstd_t, b_rstd_t, _, _ = alloc("rstd_t", [128, T], F32)
    rstd_f, b_rstd_f, _, _ = alloc("rstd_f", [128, T], F32)
    acc, b_acc, _, _ = alloc("acc", [128, T], F32)
    stats, b_stats, _, _ = alloc("stats", [128, 16], F32)
    ring, b_ring = [], []
    for i in range(RING):
        t_, b_, _, _ = alloc(f"ring{i}", [128, KC, 256], BF16)
        ring.append(t_)
        b_ring.append(b_)
    xin, b_xin = [], []
    XR = 5
    for i in range(XR):
        t_, b_, _, _ = alloc(f"xin{i}", [128, T], F32)
        xin.append(t_)
        b_xin.append(b_)
    sq, b_sq = alloc2("sq", [128, T], F32)
    A, _, A_off, _ = alloc("A", [128, KC, T], BF16, nbuf=False)
    b_A = [tr.buf(f"A{j}", A_off + j * T * 2, T * 2) for j in range(KC)]
    J, _, J_off, _ = alloc("J", [128, KC, T], BF16, nbuf=False)
    b_J = [tr.buf(f"J{j}", J_off + j * T * 2, T * 2) for j in range(KC)]
    L, _, L_off, _ = alloc("L", [128, KC, T], BF16, nbuf=False)
    b_L = [tr.buf(f"L{j}", L_off + j * T * 2, T * 2) for j in range(KC)]
    O, _, O_off, _ = alloc("O", [128, KC, T], F32, at=J_off, nbuf=False)
    b_O = [tr.buf(f"O{j}", O_off + j * T * 4, T * 4) for j in range(KC)]
    RX = cur[0]
    qT, b_qT, kT, b_kT, ktok, b_ktok, vtok, b_vtok, ltok, b_ltok = ([] for _ in range(10))
    for i in range(2):
        t_, b_, _, _ = alloc(f"qT{i}", [128, 2, T], F32); qT.append(t_); b_qT.append(b_)
        t_, b_, _, _ = alloc(f"kT{i}", [128, 2, T], F32); kT.append(t_); b_kT.append(b_)
        t_, b_, _, _ = alloc(f"ktok{i}", [128, NSUB, DK], F32); ktok.append(t_); b_ktok.append(b_)
        t_, b_, _, _ = alloc(f"vtok{i}", [128, NSUB, DV], BF16); vtok.append(t_); b_vtok.append(b_)
        t_, b_, _, _ = alloc(f"ltok{i}", [128, NSUB, DK], F32); ltok.append(t_); b_ltok.append(b_)
    etmp_off = cur[0]
    etmp, b_etmp = alloc2("etmp", [128, DK], F32)
    Ep_off = cur[0]
    Ep, b_Ep = alloc2("Ep", [128, 2, 128], F32)
    Em_off = cur[0]
    Em, b_Em = alloc2("Em", [128, 2, 128], F32)
    Er_off = cur[0]
    Er, b_Er = alloc2("Er", [128, DK], F32)
    qt, b_qt = alloc2("qt", [128, 2, 128], BF16)
    kt, b_kt = alloc2("kt", [128, 2, 128], BF16)
    kh, b_kh = alloc2("kh", [128, DK], BF16)
    AT, b_AT = alloc2("AT", [128, 128], BF16)
    on_off = cur[0]
    on, b_on = alloc2("on", [128, DV], F32)
    RX_END_GLA = cur[0]
    NOST = 6
    ost, b_ost = [], []
    ost_offs = [on_off, on_off + 2048, etmp_off, Ep_off, Em_off, Er_off]
    for i in range(NOST):
        t_, b_, _, _ = alloc(f"ost{i}", [128, T], F32, at=ost_offs[i])
        ost.append(t_)
        b_ost.append(b_)
    cur[0] = RX
    N_, _, N_off, _ = alloc("N", [128, KC, T], BF16, nbuf=False)
    b_N = [tr.buf(f"N{j}", N_off + j * T * 2, T * 2) for j in range(KC)]
    Q, _, Q_off, _ = alloc("Q", [128, 32, T], BF16, nbuf=False)
    b_Q = [tr.buf(f"Q{j}", Q_off + j * T * 2, T * 2) for j in range(32)]
    sg, b_sg = alloc2("sg", [128, T], F32)
    mt, b_mt = alloc2("mt", [128, T], F32)
    RX_END2 = cur[0]
    cur[0] = RX
    ctmp, b_ctmp, pbuf, b_pbuf, ubuf, b_ubuf = ([] for _ in range(6))
    for i in range(2):
        t_, b_, _, _ = alloc(f"ctmp{i}", [128, T], F32); ctmp.append(t_); b_ctmp.append(b_)
        t_, b_, _, _ = alloc(f"pbuf{i}", [128, T + 8], F32); pbuf.append(t_); b_pbuf.append(b_)
        t_, b_, _, _ = alloc(f"ubuf{i}", [128, T], F32); ubuf.append(t_); b_ubuf.append(b_)
    assert cur[0] <= RX + 5 * 4096, cur[0] - RX
    cur[0] = max(RX_END_GLA, RX_END2)
    assert cur[0] <= SBUF_END, cur[0]

    banks = [nc.alloc_psum_tensor(f"bank{i}", [128, 512], F32) for i in range(8)]
    b_bank = [tr.buf(f"bank{i}", excl=True) for i in range(8)]

    def newsem(name):
        return Sem(nc.alloc_semaphore(name))

    PE = Eng(nc.tensor, newsem("s_pe"), "pe", is_pe=True)
    ACT = Eng(nc.scalar, newsem("s_act"), "act")
    DVE = Eng(nc.vector, newsem("s_dve"), "dve")
    POOL = Eng(nc.gpsimd, newsem("s_pool"), "pool")
    SP = Eng(nc.sync, newsem("s_sp"), "sp")
    ring_sem = [newsem(f"s_ring{i}") for i in range(RING)]
    xin_sem = [newsem(f"s_xin{i}") for i in range(XR)]
    ost_sem = [newsem(f"s_ost{i}") for i in range(NOST)]
    cst_sem = [newsem(f"s_cst{i}") for i in range(3)]
    xres_sem = [newsem(f"s_xres{i}") for i in range(KC)]

    dbg_sem = newsem("s_dbg")
    dbg_names = []

    def dump(name, t, shape, dt, bufs, n):
        if not debug or n != 0:
            return
        d = nc.dram_tensor("dbg_" + name, list(shape), dt, kind="ExternalOutput").ap()
        dbg_names.append("dbg_" + name)
        tr.dma(SP, dbg_sem, d, t, reads=bufs)
        nc.sync.wait_ge(dbg_sem.h, dbg_sem.count)

    def mm(out, lhsT, rhs, start, stop, reads, writes, signal=None):
        if signal is None:
            signal = stop
        return tr.op(PE, lambda: nc.tensor.matmul(out, lhsT, rhs, start=start, stop=stop),
                     reads, writes, signal)

    def act(out, in_, func, reads, writes, bias=0.0, scale=1.0, accum_out=None):
        if accum_out is None:
            return tr.op(ACT, lambda: nc.scalar.activation(out, in_, func, bias=bias, scale=scale),
                         reads, writes)
        return tr.op(ACT, lambda: nc.scalar.activation(out, in_, func, bias=bias, scale=scale,
                                                       accum_out=accum_out), reads, writes)

    def tt(eng, out, in0, in1, op, reads, writes):
        return tr.op(eng, lambda: eng.h.tensor_tensor(out, in0, in1, op), reads, writes)

    def ts(eng, out, in0, s1, s2, op0, op1, reads, writes):
        return tr.op(eng, lambda: eng.h.tensor_scalar(out, in0, s1, s2, op0, op1), reads, writes)

    def stt(eng, out, in0, scalar, in1, op0, op1, reads, writes):
        return tr.op(eng, lambda: eng.h.scalar_tensor_tensor(out, in0, scalar, in1, op0, op1),
                     reads, writes)

    tr.dma(SP, cst_sem[0], cst[:, :], consts_d[:, :], writes=[b_cst])
    tr.dma(SP, cst_sem[1], wgu[0:17, :], wgu_d[:, :], writes=[b_wgu])
    tr.dma(POOL, cst_sem[2], wlr[:, :, :], wlr_d.rearrange("p (k c) -> p k c", c=16), writes=[b_wlr])
    ident = cst[:, C_ID:C_ID + 128]
    triU = cst[:, C_TRIU:C_TRIU + 128]
    triS = cst[:, C_TRIS:C_TRIS + 128]
    maskc = cst[:, C_MASK:C_MASK + 128]
    onesm = cst[:, C_ONES:C_ONES + 128]
    ggla_b = cst[:, C_GGLA:C_GGLA + 512]
    tr.op(DVE, lambda: nc.vector.memset(lraug[:, :], 1.0), writes=[b_lraug])

    total_slots = n_tiles * NSLOT
    wstate = {"issued": 0, "consumed": 0}
    worder = []

    def w_issue():
        g = wstate["issued"]
        rs = g % RING
        tr.dma(POOL, ring_sem[rs], ring[rs][:, :, :],
               wts[g % NSLOT].rearrange("p (k c) -> p k c", c=256), writes=[b_ring[rs]])
        wstate["issued"] += 1

    def w_next(key):
        c = wstate["consumed"]
        if c < NSLOT:
            worder.append(key)
        else:
            assert worder[c % NSLOT] == key, (c, key)
        while wstate["issued"] < total_slots and wstate["issued"] <= c - 1 + RING:
            w_issue()
        wstate["consumed"] += 1
        rs = c % RING
        return ring[rs], b_ring[rs]

    xorder = []
    for n in range(n_tiles):
        for p in range(2):
            for j in range(KC):
                xorder.append((n, j))
    xstate = {"issued": 0, "consumed": 0}

    def x_issue():
        g = xstate["issued"]
        n, j = xorder[g]
        s = g % XR
        tr.dma(SP, xin_sem[s], xin[s][:, :], xT[j * 128:(j + 1) * 128, n * T:(n + 1) * T],
               writes=[b_xin[s]])
        xstate["issued"] += 1

    def x_next():
        c = xstate["consumed"]
        while xstate["issued"] < len(xorder) and xstate["issued"] <= c - 1 + XR:
            x_issue()
        xstate["consumed"] += 1
        return xin[c % XR], b_xin[c % XR]

    cnt = {"sq": 0, "pa": 0, "pw": 0, "chunk": 0, "ost": 0}

    def nbank():
        b = cnt["pa"] % 2
        cnt["pa"] += 1
        return b

    def nbank_w():
        b_ = cnt["pw"] % 4
        cnt["pw"] += 1
        return b_

    def nsq():
        s = cnt["sq"] % 2
        cnt["sq"] += 1
        return s

    def stat_add(j, src, bsrc):
        if j == 0:
            act(acc[:, :], src, AF.Square, [bsrc], [b_acc])
        else:
            s = nsq()
            act(sq[s][:, :], src, AF.Square, [bsrc], [b_sq[s]])
            tt(DVE, acc[:, :], acc[:, :], sq[s][:, :], ALU.add, [b_acc, b_sq[s]], [b_acc])

    def stat_finish(dst, bdst):
        mm(banks[7][:, :], onesm, acc[:, :], True, True, [b_acc, b_cst], [b_bank[7]])
        act(dst[:, :], banks[7][:, :], AF.Ln, [b_bank[7]], [bdst], bias=EPS)
        act(dst[:, :], dst[:, :], AF.Exp, [bdst], [bdst], scale=-0.5)

    def proj_fm(slot, bslot, u, rhs_t, rhs_b, nk, bank, first=True, last=True, kbase=0):
        for kc in range(nk):
            mm(banks[bank][:, :], slot[:, kc, u * 128:(u + 1) * 128], rhs_t[:, kbase + kc, :],
               first and kc == 0, last and kc == nk - 1, [bslot, rhs_b[kbase + kc]], [b_bank[bank]],
               signal=(kc == nk - 1))

    def g_stats(n):
        for j in range(KC):
            t_, b_ = x_next()
            stat_add(j, t_[:, :], b_)
            yield
        stat_finish(rstd_t, b_rstd_t)
        yield

    def g_norm(n):
        for j in range(KC):
            t_, b_ = x_next()
            stt(DVE, A[:, j, :], t_[:, :], cst[:, C_GMIX + j:C_GMIX + j + 1], rstd_t[:, :],
                ALU.mult, ALU.mult, [b_, b_rstd_t, b_cst], [b_A[j]])
            yield

    def g_r(slots, nb=None):
        nb = nb or nbank
        for s in slots:
            slot, bslot = w_next(("in", 0, R_OFF + s * 256))
            for u in range(2):
                j = 2 * s + u
                bank = nb()
                proj_fm(slot, bslot, u, A, b_A, KC, bank)
                si = nsq()
                act(sq[si][:, :], banks[bank][:, :], AF.Exp, [b_bank[bank]], [b_sq[si]], scale=-1.0)
                act(sq[si][:, :], sq[si][:, :], AF.Ln, [b_sq[si]], [b_sq[si]], bias=1.0)
                act(sq[si][:, :], sq[si][:, :], AF.Exp, [b_sq[si]], [b_sq[si]], scale=-1.0)
                tt(DVE, J[:, j, :], banks[bank][:, :], sq[si][:, :], ALU.mult,
                   [b_bank[bank], b_sq[si]], [b_J[j]])
                yield

    def g_proj_head(h, nb=None):
        nb = nb or nbank
        hb = h % 2
        slot, bslot = w_next(("in", 0, K_OFF + h * 256))
        for u in range(2):
            bank = nb()
            proj_fm(slot, bslot, u, A, b_A, KC, bank)
            act(kT[hb][:, u, :], banks[bank][:, :], AF.Copy, [b_bank[bank]], [b_kT[hb]])
            yield
        slot, bslot = w_next(("in", 0, Q_OFF + h * 256))
        for u in range(2):
            bank = nb()
            proj_fm(slot, bslot, u, A, b_A, KC, bank)
            act(qT[hb][:, u, :], banks[bank][:, :], AF.Copy, [b_bank[bank]], [b_qT[hb]])
            yield
        for sp in range(2):
            bank = nb()
            for s2 in range(2):
                sub = sp * 2 + s2
                for u in range(2):
                    c0 = s2 * 256 + u * 128
                    tr.op(PE, lambda: nc.tensor.transpose(banks[bank][:, c0:c0 + 128],
                                                          kT[hb][:, u, sub * 128:(sub + 1) * 128], ident),
                          [b_kT[hb], b_cst], [b_bank[bank]], signal=(s2 == 1 and u == 1))
            tr.op(DVE, lambda: nc.vector.tensor_copy(
                ktok[hb][:, sp * 2:sp * 2 + 2, :],
                banks[bank][:, :].rearrange("p (a b) -> p a b", a=2)),
                [b_bank[bank]], [b_ktok[hb]])
        yield
        for half in range(2):
            slot, bslot = w_next(("in", 0, V_OFF + h * 512 + half * 256))
            for sp in range(2):
                bank = nb()
                for s2 in range(2):
                    sub = sp * 2 + s2
                    for kc in range(KC):
                        mm(banks[bank][:, s2 * 256:(s2 + 1) * 256],
                           A[:, kc, sub * 128:(sub + 1) * 128], slot[:, kc, :],
                           kc == 0, kc == KC - 1, [bslot, b_A[kc]], [b_bank[bank]],
                           signal=(kc == KC - 1))
                tr.op(DVE, lambda: nc.vector.tensor_copy(
                    vtok[hb][:, sp * 2:sp * 2 + 2, half * 256:(half + 1) * 256],
                    banks[bank][:, :].rearrange("p (a b) -> p a b", a=2)),
                    [b_bank[bank]], [b_vtok[hb]])
                yield
        for sp in range(2):
            bank = nb()
            for s2 in range(2):
                sub = sp * 2 + s2
                mm(banks[bank][:, s2 * 256:(s2 + 1) * 256], lraug[0:17, sub * 128:(sub + 1) * 128],
                   wgu[0:17, h * 256:(h + 1) * 256], True, True, [b_lraug, b_wgu], [b_bank[bank]])
            lv = ltok[hb][:, sp * 2:sp * 2 + 2, :]
            act(lv, banks[bank][:, :].rearrange("p (a b) -> p a b", a=2), AF.Exp,
                [b_bank[bank]], [b_ltok[hb]], scale=-1.0)
            act(lv, lv, AF.Ln, [b_ltok[hb]], [b_ltok[hb]], bias=1.0)
            yield

    def g_gla(h, n):
        hb = h % 2
        cbs = []
        for c in range(NSUB):
            cbs.append(cnt["chunk"] % 2)
            cnt["chunk"] += 1

        def s1(c):
            cb = cbs[c]
            csl = slice(c * 128, (c + 1) * 128)
            for j in range(2):
                mm(banks[2][:, j * 128:(j + 1) * 128], ltok[hb][:, c, j * 128:(j + 1) * 128], triU,
                   True, True, [b_ltok[hb], b_cst], [b_bank[2]])
            mm(banks[2][:, 256:512], triS, ltok[hb][:, c, :], True, True,
               [b_ltok[hb], b_cst], [b_bank[2]])
            cum2 = banks[2][:, 0:256].rearrange("p (a b) -> p a b", a=2)
            act(Ep[cb][:, :, :], cum2, AF.Exp, [b_bank[2]], [b_Ep[cb]])
            act(Em[cb][:, :, :], cum2, AF.Exp, [b_bank[2]], [b_Em[cb]], scale=-1.0)
            act(Er[cb][:, :], banks[2][:, 256:512], AF.Exp, [b_bank[2]], [b_Er[cb]])
            stt(DVE, qt[cb][:, :, :], qT[hb][:, :, csl], DK ** -0.5, Ep[cb][:, :, :],
                ALU.mult, ALU.mult, [b_qT[hb], b_Ep[cb]], [b_qt[cb]])
            tt(DVE, kt[cb][:, :, :], kT[hb][:, :, csl], Em[cb][:, :, :], ALU.mult,
               [b_kT[hb], b_Em[cb]], [b_kt[cb]])
            tt(DVE, kh[cb][:, :], ktok[hb][:, c, :], Er[cb][:, :], ALU.mult,
               [b_ktok[hb], b_Er[cb]], [b_kh[cb]])

        s1(0)
        yield
        for c in range(NSUB):
            cb = cbs[c]
            csl = slice(c * 128, (c + 1) * 128)
            for j in range(2):
                mm(banks[3][:, 0:128], kt[cb][:, j, :], qt[cb][:, j, :], j == 0, j == 1,
                   [b_kt[cb], b_qt[cb]], [b_bank[3]])
            tt(DVE, AT[cb][:, :], banks[3][:, 0:128], maskc, ALU.mult,
               [b_bank[3], b_cst], [b_AT[cb]])
            for j in range(2):
                mm(banks[5 + j][:, :], kh[cb][:, j * 128:(j + 1) * 128], vtok[hb][:, c, :], True, True,
                   [b_kh[cb], b_vtok[hb]], [b_bank[5 + j]])
            yield
            mm(banks[4][:, :], qt[cb][:, 0, :], state_b[2 * h][:, :], True, False,
               [b_qt[cb], b_sb[2 * h]], [b_bank[4]], signal=False)
            mm(banks[4][:, :], qt[cb][:, 1, :], state_b[2 * h + 1][:, :], False, False,
               [b_qt[cb], b_sb[2 * h + 1]], [b_bank[4]], signal=False)
            mm(banks[4][:, :], AT[cb][:, :], vtok[hb][:, c, :], False, True,
               [b_AT[cb], b_vtok[hb]], [b_bank[4]])
            for j in range(2):
                si = 2 * h + j
                stt(DVE, state_f[si][:, :], state_f[si][:, :], Ep[cb][:, j, 127:128], banks[5 + j][:, :],
                    ALU.mult, ALU.add, [b_sf[si], b_Ep[cb], b_bank[5 + j]], [b_sf[si]])
            k0 = cb * 4
            act(on[cb][:, :], banks[4][:, :], AF.Square, [b_bank[4]], [b_on[cb], b_stats],
                scale=float(DV ** -0.5), accum_out=stats[:, k0:k0 + 1])
            act(stats[:, k0 + 1:k0 + 2], stats[:, k0:k0 + 1], AF.Ln, [b_stats], [b_stats], bias=EPS)
            act(stats[:, k0 + 2:k0 + 3], stats[:, k0 + 1:k0 + 2], AF.Exp, [b_stats], [b_stats], scale=-0.5)
            stt(DVE, on[cb][:, :], banks[4][:, :], stats[:, k0 + 2:k0 + 3], ggla_b,
                ALU.mult, ALU.mult, [b_bank[4], b_stats, b_cst], [b_on[cb]])
            for j in range(2):
                si = 2 * h + j
                act(state_b[si][:, :], state_f[si][:, :], AF.Copy, [b_sf[si]], [b_sb[si]])
            yield
            if c + 1 < NSUB:
                s1(c + 1)
                yield
            if debug and h == 0 and c == 0:
                dump("on", on[cb][:, :], [128, DV], F32, [b_on[cb]], n)
            for i in range(4):
                tr.op(PE, lambda i=i: nc.tensor.transpose(banks[7][:, i * 128:(i + 1) * 128],
                                                          on[cb][:, i * 128:(i + 1) * 128], ident),
                      [b_on[cb], b_cst], [b_bank[7]], signal=(i == 3))
            tt(DVE, J[:, 4 * h:4 * h + 4, csl], banks[7][:, :].rearrange("p (a b) -> p a b", a=4),
               J[:, 4 * h:4 * h + 4, csl], ALU.mult,
               [b_bank[7]] + b_J[4 * h:4 * h + 4], b_J[4 * h:4 * h + 4])
            yield

    def g_conv():
        for jp in range(8):
            slot, bslot = w_next(("in", 0, CC_OFF + jp * 256))
            for u in range(2):
                bank = nbank()
                proj_fm(slot, bslot, u, A, b_A, KC, bank)
                act(ctmp[u][:, :], banks[bank][:, :], AF.Copy, [b_bank[bank]], [b_ctmp[u]])
                yield
            slot, bslot = w_next(("in", 0, CX_OFF + jp * 256))
            for u in range(2):
                j = 2 * jp + u
                bank = nbank()
                proj_fm(slot, bslot, u, A, b_A, KC, bank)
                p_ = pbuf[u]
                tr.op(ACT, lambda: nc.scalar.copy(p_[:, 0:2], halo[:, j, :]), [b_halo], [b_pbuf[u]])
                tt(DVE, p_[:, 2:2 + T], banks[bank][:, :], ctmp[u][:, :], ALU.mult,
                   [b_bank[bank], b_ctmp[u]], [b_pbuf[u]])
                tr.op(ACT, lambda: nc.scalar.copy(halo[:, j, :], p_[:, T:T + 2]), [b_pbuf[u]], [b_halo])
                cw0 = cst[:, C_CW + 3 * j:C_CW + 3 * j + 1]
                cw1 = cst[:, C_CW + 3 * j + 1:C_CW + 3 * j + 2]
                cw2 = cst[:, C_CW + 3 * j + 2:C_CW + 3 * j + 3]
                u_ = ubuf[u]
                ts(DVE, u_[:, :], p_[:, 2:2 + T], cw2, None, ALU.mult, ALU.bypass,
                   [b_pbuf[u], b_cst], [b_ubuf[u]])
                stt(DVE, u_[:, :], p_[:, 1:1 + T], cw1, u_[:, :], ALU.mult, ALU.add,
                    [b_pbuf[u], b_cst, b_ubuf[u]], [b_ubuf[u]])
                stt(DVE, u_[:, :], p_[:, 0:T], cw0, u_[:, :], ALU.mult, ALU.add,
                    [b_pbuf[u], b_cst, b_ubuf[u]], [b_ubuf[u]])
                yield
            slot, bslot = w_next(("in", 0, CB_OFF + jp * 256))
            for u in range(2):
                j = 2 * jp + u
                bank = nbank()
                proj_fm(slot, bslot, u, A, b_A, KC, bank)
                tt(DVE, L[:, j, :], banks[bank][:, :], ubuf[u][:, :], ALU.mult,
                   [b_bank[bank], b_ubuf[u]], [b_L[j]])
                yield

    def drain(g):
        for _ in g:
            pass

    def chain(*gs):
        for g in gs:
            yield from g

    def interleave(main, filler, k=1):
        for _ in main:
            for _i in range(k):
                next(filler, None)

    drain(g_stats(0))
    drain(g_norm(0))
    for n in range(n_tiles):
        tcol = slice(n * T, (n + 1) * T)
        if n % tiles_per_seq == 0:
            for i in range(8):
                tr.op(DVE, lambda i=i: nc.vector.memset(state_f[i][:, :], 0.0), writes=[b_sf[i]])
                tr.op(DVE, lambda i=i: nc.vector.memset(state_b[i][:, :], 0.0), writes=[b_sb[i]])
            tr.op(DVE, lambda: nc.vector.memset(halo[:, :, :], 0.0), writes=[b_halo])
        dump("A", A[:, :, :], [128, KC, T], BF16, b_A, n)

        for kc in range(KC):
            mm(banks[0][0:16, :], wlr[:, kc, :], A[:, kc, :], kc == 0, kc == KC - 1,
               [b_wlr, b_A[kc]], [b_bank[0]], signal=(kc == KC - 1))
        act(lraug[0:16, :], banks[0][0:16, :], AF.Copy, [b_bank[0]], [b_lraug])

        drain(g_proj_head(0, nbank_w))
        drain(g_r([0, 1], nbank_w))
        conv = g_conv()
        for h in range(NH):
            if h < NH - 1:
                filler = chain(g_proj_head(h + 1), g_r([2 * h + 2, 2 * h + 3]))
                interleave(g_gla(h, n), filler)
                drain(filler)
            else:
                interleave(g_gla(h, n), conv)
        dump("Joa", J[:, :, :], [128, KC, T], BF16, b_J, n)
        drain(conv)
        dump("L", L[:, :, :], [128, KC, T], BF16, b_L, n)

        for jp in range(8):
            slot, bslot = w_next(("in", 0, GA_OFF + jp * 256))
            for u in range(2):
                bank = nbank_w()
                proj_fm(slot, bslot, u, A, b_A, KC, bank)
                act(sg[u][:, :], banks[bank][:, :], AF.Sigmoid, [b_bank[bank]], [b_sg[u]])
            slot, bslot = w_next(("gla", 0, jp * 256))
            for u in range(2):
                bank = nbank_w()
                proj_fm(slot, bslot, u, J, b_J, KC, bank)
                tt(DVE, mt[u][:, :], banks[bank][:, :], sg[u][:, :], ALU.mult,
                   [b_bank[bank], b_sg[u]], [b_mt[u]])
            slot, bslot = w_next(("in", 0, GB_OFF + jp * 256))
            for u in range(2):
                bank = nbank_w()
                proj_fm(slot, bslot, u, A, b_A, KC, bank)
                act(sg[u][:, :], banks[bank][:, :], AF.Sigmoid, [b_bank[bank]], [b_sg[u]])
            slot, bslot = w_next(("conv", 0, jp * 256))
            for u in range(2):
                j = 2 * jp + u
                bank = nbank_w()
                proj_fm(slot, bslot, u, L, b_L, KC, bank)
                tt(DVE, sg[u][:, :], banks[bank][:, :], sg[u][:, :], ALU.mult,
                   [b_bank[bank], b_sg[u]], [b_sg[u]])
                tt(DVE, N_[:, j, :], sg[u][:, :], mt[u][:, :], ALU.add,
                   [b_sg[u], b_mt[u]], [b_N[j]])
        dump("N", N_[:, :, :], [128, KC, T], BF16, b_N, n)

        for j in range(KC):
            tr.dma(SP, xres_sem[j], O[:, j, :], xT[j * 128:(j + 1) * 128, tcol], writes=[b_O[j]])
        for jp in range(8):
            slot, bslot = w_next(("wo", 0, jp * 256))
            for u in range(2):
                j = 2 * jp + u
                bank = nbank_w()
                proj_fm(slot, bslot, u, N_, b_N, KC, bank)
                tt(DVE, O[:, j, :], banks[bank][:, :], O[:, j, :], ALU.add,
                   [b_bank[bank], b_O[j]], [b_O[j]])
                act(A[:, j, :], O[:, j, :], AF.Copy, [b_O[j], b_cst], [b_A[j]],
                    scale=cst[:, C_GFFN + j:C_GFFN + j + 1])
                stat_add(j, O[:, j, :], b_O[j])
        dump("O1", O[:, :, :], [128, KC, T], F32, b_O, n)

        has_next = n + 1 < n_tiles
        nstats = g_stats(n + 1) if has_next else iter(())
        nnorm = g_norm(n + 1) if has_next else iter(())
        for half in range(2):
            for s in range(16):
                slot, bslot = w_next(("up", 0, half * 4096 + s * 256))
                ubk = [nbank_w(), nbank_w()]
                if half == 0 and s == 0:
                    for u in range(2):
                        proj_fm(slot, bslot, u, A, b_A, KC, ubk[u])
                    stat_finish(rstd_f, b_rstd_f)
                for u in range(2):
                    f = 2 * s + u
                    bank = ubk[u]
                    if not (half == 0 and s == 0):
                        proj_fm(slot, bslot, u, A, b_A, KC, bank)
                    si = nsq()
                    tt(DVE, sq[si][:, :], banks[bank][:, :], rstd_f[:, :], ALU.mult,
                       [b_bank[bank], b_rstd_f], [b_sq[si]])
                    stt(DVE, Q[:, f, :], sq[si][:, :], 0.0, sq[si][:, :], ALU.max, ALU.mult,
                        [b_sq[si]], [b_Q[f]])
                if half == 1 and s < 8:
                    next(nstats, None)
                    next(nstats, None)
            if half == 1:
                drain(nstats)
            for jp in range(8):
                bk = [nbank_w(), nbank_w()]
                s0, bs0 = w_next(("down", (half * 2) * 2048, jp * 256))
                for u in range(2):
                    proj_fm(s0, bs0, u, Q, b_Q, KC, bk[u], first=True, last=False, kbase=0)
                s1, bs1 = w_next(("down", (half * 2 + 1) * 2048, jp * 256))
                for u in range(2):
                    j = 2 * jp + u
                    proj_fm(s1, bs1, u, Q, b_Q, KC, bk[u], first=False, last=True, kbase=16)
                    tt(DVE, O[:, j, :], banks[bk[u]][:, :], O[:, j, :], ALU.add,
                       [b_bank[bk[u]], b_O[j]], [b_O[j]])
                    if half == 1:
                        stat_add(j, O[:, j, :], b_O[j])
                if half == 1 and jp < 4:
                    for _i in range(4):
                        next(nnorm, None)
        drain(nnorm)
        dump("O2", O[:, :, :], [128, KC, T], F32, b_O, n)

        stat_finish(rstd_f, b_rstd_f)
        for j in range(KC):
            s = cnt["ost"] % NOST
            cnt["ost"] += 1
            stt(DVE, ost[s][:, :], O[:, j, :], cst[:, C_GFIN + j:C_GFIN + j + 1], rstd_f[:, :],
                ALU.mult, ALU.mult, [b_O[j], b_rstd_f, b_cst], [b_ost[s]])
            tr.dma(SP, ost_sem[s], outT[j * 128:(j + 1) * 128, tcol], ost[s][:, :], reads=[b_ost[s]])

    for s in range(NOST):
        nc.sync.wait_ge(ost_sem[s].h, ost_sem[s].count)
    assert wstate["consumed"] == total_slots and len(worder) == NSLOT
    DBG_NAMES[:] = dbg_names
    WORDER[:] = worder
    return nc


def _slotify(w):
    return np.ascontiguousarray(w.reshape(KC, 128, 256).transpose(1, 0, 2)).reshape(128, KC * 256)


def prep_weights(order, w_in, w_gla_out, w_conv_out, w_o, w_ffn_up, w_ffn_down):
    mats = {"in": w_in, "gla": w_gla_out, "conv": w_conv_out, "wo": w_o, "up": w_ffn_up,
            "down": w_ffn_down}
    out = np.empty((NSLOT, 128, KC * 256), np.float32)
    for i, (name, r0, c0) in enumerate(order):
        out[i] = _slotify(mats[name][r0:r0 + 2048, c0:c0 + 256])
    return out


def prep_consts(g_mix, g_ffn, g_final, g_gla_norm, conv_w):
    c = np.zeros((128, NCONST), np.float32)
    idx = np.arange(128)
    c[:, C_ID:C_ID + 128] = np.eye(128, dtype=np.float32)
    s_le_t = (idx[:, None] <= idx[None, :])
    c[:, C_TRIU:C_TRIU + 128] = np.where(s_le_t, -1.0 / 16.0, 0.0)
    c[:, C_TRIS:C_TRIS + 128] = np.where(~s_le_t, -1.0 / 16.0, 0.0)
    c[:, C_MASK:C_MASK + 128] = np.where(s_le_t, 1.0, 0.0)
    c[:, C_ONES:C_ONES + 128] = 1.0 / D
    c[:, C_GMIX:C_GMIX + 16] = g_mix.reshape(KC, 128).T
    c[:, C_GFFN:C_GFFN + 16] = g_ffn.reshape(KC, 128).T
    c[:, C_GFIN:C_GFIN + 16] = g_final.reshape(KC, 128).T
    c[:, C_GGLA:C_GGLA + 512] = np.broadcast_to(g_gla_norm.reshape(1, DV), (128, DV))
    c[:, C_CW:C_CW + 48] = conv_w.reshape(3, KC, 128).transpose(2, 1, 0).reshape(128, 48)
    return c


def prep_small(w_in, w_gate_up, b_gate):
    wgu = np.concatenate([w_gate_up, b_gate.reshape(1, 1024)], axis=0).astype(np.float32)
    wlr = np.ascontiguousarray(
        w_in[:, LR_OFF:LR_OFF + 16].reshape(KC, 128, 16).transpose(1, 0, 2)).reshape(128, KC * 16)
    return np.ascontiguousarray(wgu), wlr


def kernel(x, g_mix, w_in, w_gate_up, b_gate, g_gla_norm, w_gla_out, conv_w, w_conv_out, w_o,
           g_ffn, w_ffn_up, w_ffn_down, g_final):
    x = np.asarray(x, np.float32)
    f = lambda a: np.asarray(a, np.float32)
    seq_per_core = BATCH // N_CORES
    tiles_per_seq = SEQ // T
    n_tiles = seq_per_core * tiles_per_seq
    nc = build_nc(n_tiles, tiles_per_seq)
    wts = prep_weights(list(WORDER), f(w_in)[0], f(w_gla_out)[0], f(w_conv_out)[0], f(w_o)[0],
                       f(w_ffn_up)[0], f(w_ffn_down)[0])
    consts = prep_consts(f(g_mix)[0], f(g_ffn)[0], f(g_final), f(g_gla_norm)[0], f(conv_w)[0])
    wgu, wlr = prep_small(f(w_in)[0], f(w_gate_up)[0], f(b_gate)[0])
    in_maps = []
    for c in range(N_CORES):
        xc = x[c * seq_per_core:(c + 1) * seq_per_core].reshape(seq_per_core * SEQ, D)
        in_maps.append({"xT": np.ascontiguousarray(xc.T), "wts": wts, "consts": consts,
                        "wgu": wgu, "wlr": wlr})
    res = run_bass_kernel_spmd(nc, in_maps, core_ids=list(range(N_CORES)))
    out = np.empty((BATCH, SEQ, D), np.float32)
    for c in range(N_CORES):
        oT = np.asarray(res.results[c]["outT"])
        out[c * seq_per_core:(c + 1) * seq_per_core] = oT.T.reshape(seq_per_core, SEQ, D)
    return out
```
